# Optimizing a Trainium2 kernel written in Bass

```python
import jax, jax.numpy as jnp
from jax import lax
import numpy as np

D_MODEL = 2048
BATCH = 8
SEQ = 2048
DEPTH = 4

HEAD_DIM = 128
DIL_GROUPS = ((128, 1), (512, 4), (2048, 16))
HEADS_PER_DIL = 4
N_DIL_HEADS = HEADS_PER_DIL * len(DIL_GROUPS)
N_FOX_HEADS = 4
N_HEADS = N_DIL_HEADS + N_FOX_HEADS
ATTN_WIDTH = N_HEADS * HEAD_DIM
BRANCH_A_WIDTH = HEADS_PER_DIL * HEAD_DIM
BRANCH_B_WIDTH = N_FOX_HEADS * HEAD_DIM
IN_COLS = 3 * ATTN_WIDTH + N_FOX_HEADS
D_FF = 4 * D_MODEL
PLE_DIM = 256
ROPE_THETA = 500000.0
ROPE_DIM = HEAD_DIM // 4
BLOCK = 128
NORM_EPS = 1e-6

kernel_name = "hybrid_dilated_fox_gated_block"


def rms_norm(x, g):
    xf = x.astype(jnp.float32)
    y = xf * lax.rsqrt(jnp.mean(xf * xf, axis=-1, keepdims=True) + NORM_EPS)
    return (y * g.astype(jnp.float32)).astype(x.dtype)


def partial_rope(x):
    S = x.shape[1]
    half = ROPE_DIM // 2
    inv = ROPE_THETA ** (-jnp.arange(half, dtype=jnp.float32) / half)
    ang = jnp.arange(S, dtype=jnp.float32)[:, None] * inv[None, :]
    cos = jnp.cos(ang)[None, :, None, :]
    sin = jnp.sin(ang)[None, :, None, :]
    xr = x[..., :ROPE_DIM].astype(jnp.float32)
    x1, x2 = xr[..., :half], xr[..., half:]
    rot = jnp.concatenate([x1 * cos - x2 * sin, x2 * cos + x1 * sin], axis=-1).astype(x.dtype)
    return jnp.concatenate([rot, x[..., ROPE_DIM:]], axis=-1)


def dilated_group_attention(q, k, v, window, dilation):
    B, S, H, Dh = q.shape
    span = window // dilation
    L = S // dilation
    nb = -(-L // BLOCK)
    Lp = nb * BLOCK

    def to_blocks(t):
        t = t.reshape(B, L, dilation, H, Dh).transpose(0, 2, 1, 3, 4)
        t = jnp.pad(t, ((0, 0), (0, 0), (0, Lp - L), (0, 0), (0, 0)))
        return t.reshape(B, dilation, nb, BLOCK, H, Dh)

    def with_prev(t):
        prev = jnp.pad(t[:, :, :-1], ((0, 0), (0, 0), (1, 0), (0, 0), (0, 0), (0, 0)))
        return jnp.concatenate([prev, t], axis=3)

    qb = to_blocks(q)
    kb = with_prev(to_blocks(k))
    vb = with_prev(to_blocks(v))
    s = jnp.einsum("brnqhd,brnkhd->brnhqk", qb, kb).astype(jnp.float32) * (Dh ** -0.5)
    qi = jnp.arange(BLOCK)[:, None]
    ki = jnp.arange(2 * BLOCK)[None, :]
    dist = BLOCK + qi - ki
    band = (dist >= 0) & (dist <= span)
    key_exists = (jnp.arange(nb) > 0)[:, None, None] | (ki >= BLOCK)[None]
    mask = (band[None] & key_exists)[:, None]
    s = jnp.where(mask, s, -jnp.inf)
    lse = jax.nn.logsumexp(s, axis=-1)
    prob = jnp.exp(s - lse[..., None]).astype(v.dtype)
    o = jnp.einsum("brnhqk,brnkhd->brnqhd", prob, vb)
    o = o.reshape(B, dilation, Lp, H, Dh)[:, :, :L].transpose(0, 2, 1, 3, 4).reshape(B, S, H, Dh)
    lse = lse.transpose(0, 1, 2, 4, 3).reshape(B, dilation, Lp, H)[:, :, :L]
    lse = lse.transpose(0, 2, 1, 3).reshape(B, S, H)
    return o, lse


def dilated_mixture(q, k, v):
    outs, lses = [], []
    for g, (window, dilation) in enumerate(DIL_GROUPS):
        sl = slice(g * HEADS_PER_DIL, (g + 1) * HEADS_PER_DIL)
        o, l = dilated_group_attention(q[:, :, sl], k[:, :, sl], v[:, :, sl], window, dilation)
        outs.append(o)
        lses.append(l)
    o = jnp.stack(outs, axis=0)
    w = jax.nn.softmax(jnp.stack(lses, axis=0), axis=0)
    return jnp.sum(w[..., None].astype(o.dtype) * o, axis=0)


def forgetting_attention(q, k, v, f_logit):
    B, S, H, Dh = q.shape
    nb = S // BLOCK
    c = jnp.cumsum(jax.nn.log_sigmoid(f_logit.astype(jnp.float32)), axis=1)
    c_keys = c.transpose(0, 2, 1)[:, :, None, :]
    qb = q.reshape(B, nb, BLOCK, H, Dh).transpose(1, 0, 2, 3, 4)
    cb = c.reshape(B, nb, BLOCK, H).transpose(1, 0, 3, 2)
    kpos = jnp.arange(S)
    scale = Dh ** -0.5

    def one_block(args):
        j, qj, cj = args
        s = jnp.einsum("bqhd,bkhd->bhqk", qj, k).astype(jnp.float32) * scale
        s = s + cj[..., None] - c_keys
        qpos = j * BLOCK + jnp.arange(BLOCK)
        s = jnp.where(kpos[None, :] <= qpos[:, None], s, -jnp.inf)
        prob = jax.nn.softmax(s, axis=-1).astype(v.dtype)
        return jnp.einsum("bhqk,bkhd->bqhd", prob, v)

    out = lax.map(one_block, (jnp.arange(nb), qb, cb))
    return out.transpose(1, 0, 2, 3, 4).reshape(B, S, H, Dh)


def hybrid_layer(h, p_i, g_mix, w_in, b_f, w_gate, b_gate, w_br_a, w_br_b, w_o,
                 g_mlp, w_up, w_down, g_ple, w_ple, w_ple_gate):
    B, S, _ = h.shape
    u = rms_norm(h, g_mix)
    z = u @ w_in
    q = z[..., :ATTN_WIDTH].reshape(B, S, N_HEADS, HEAD_DIM)
    k = z[..., ATTN_WIDTH:2 * ATTN_WIDTH].reshape(B, S, N_HEADS, HEAD_DIM)
    v = z[..., 2 * ATTN_WIDTH:3 * ATTN_WIDTH].reshape(B, S, N_HEADS, HEAD_DIM)
    f_logit = z[..., 3 * ATTN_WIDTH:] + b_f

    ya = dilated_mixture(partial_rope(q[:, :, :N_DIL_HEADS]), partial_rope(k[:, :, :N_DIL_HEADS]),
                         v[:, :, :N_DIL_HEADS])
    ya = ya.reshape(B, S, BRANCH_A_WIDTH) @ w_br_a
    yb = forgetting_attention(q[:, :, N_DIL_HEADS:], k[:, :, N_DIL_HEADS:], v[:, :, N_DIL_HEADS:], f_logit)
    yb = yb.reshape(B, S, BRANCH_B_WIDTH) @ w_br_b

    gates = jax.nn.sigmoid(u @ w_gate + b_gate)
    merged = gates[..., :D_MODEL] * ya + gates[..., D_MODEL:] * yb
    h = h + merged @ w_o

    m = rms_norm(h, g_mlp)
    h = h + jnp.square(jax.nn.relu(m @ w_up)) @ w_down

    ple_gate = jax.nn.sigmoid(rms_norm(h, g_ple) @ w_ple_gate)
    h = h + ple_gate * (p_i @ w_ple)
    return h


def setup_inputs(seed: int = 0) -> dict:
    key = jax.random.key(seed)
    ks = jax.random.split(key, 20)
    f32 = jnp.float32

    def w(k, shape, fan_in):
        return jax.random.normal(k, shape, f32) * (fan_in ** -0.5)

    def gain(k, shape):
        return 1.0 + 0.02 * jax.random.normal(k, shape, f32)

    return {
        "x": jax.random.normal(ks[0], (BATCH, SEQ, D_MODEL), f32),
        "p": jax.random.normal(ks[1], (DEPTH, BATCH, SEQ, PLE_DIM), f32),
        "g_mix": gain(ks[2], (DEPTH, D_MODEL)),
        "w_in": w(ks[3], (DEPTH, D_MODEL, IN_COLS), D_MODEL),
        "b_f": 3.0 + 0.1 * jax.random.normal(ks[4], (DEPTH, N_FOX_HEADS), f32),
        "w_gate": w(ks[5], (DEPTH, D_MODEL, 2 * D_MODEL), D_MODEL),
        "b_gate": 0.1 * jax.random.normal(ks[6], (DEPTH, 2 * D_MODEL), f32),
        "w_br_a": w(ks[7], (DEPTH, BRANCH_A_WIDTH, D_MODEL), BRANCH_A_WIDTH),
        "w_br_b": w(ks[8], (DEPTH, BRANCH_B_WIDTH, D_MODEL), BRANCH_B_WIDTH),
        "w_o": w(ks[9], (DEPTH, D_MODEL, D_MODEL), D_MODEL),
        "g_mlp": gain(ks[10], (DEPTH, D_MODEL)),
        "w_up": w(ks[11], (DEPTH, D_MODEL, D_FF), D_MODEL),
        "w_down": w(ks[12], (DEPTH, D_FF, D_MODEL), D_FF),
        "g_ple": gain(ks[13], (DEPTH, D_MODEL)),
        "w_ple": w(ks[14], (DEPTH, PLE_DIM, D_MODEL), PLE_DIM),
        "w_ple_gate": w(ks[15], (DEPTH, D_MODEL, D_MODEL), D_MODEL),
        "g_final": gain(ks[16], (D_MODEL,)),
    }


def reference(x, p, g_mix, w_in, b_f, w_gate, b_gate, w_br_a, w_br_b, w_o,
              g_mlp, w_up, w_down, g_ple, w_ple, w_ple_gate, g_final):
    h = x
    for i in range(DEPTH):
        h = hybrid_layer(h, p[i], g_mix[i], w_in[i], b_f[i], w_gate[i], b_gate[i], w_br_a[i], w_br_b[i],
                         w_o[i], g_mlp[i], w_up[i], w_down[i], g_ple[i], w_ple[i], w_ple_gate[i])
    return rms_norm(h, g_final)
```

```python
import os
import numpy as np
import concourse.bass as bass
import concourse.mybir as mybir
from concourse.bass_utils import run_bass_kernel_spmd

F32 = mybir.dt.float32
BF16 = mybir.dt.bfloat16
AF = mybir.ActivationFunctionType
ALU = mybir.AluOpType

D = 2048
S = 2048
L = 4
NCH = 16
NT = 4
TT = 512
DFF = 8192
EPS = 1e-6
SCALE = 128 ** -0.5
DILS = (1, 4, 16)

ENGS = ("pe", "act", "dve", "pool", "sp")
SAME_ENG_SYNC = {"act", "dve", "pool"}


class Buf:
    __slots__ = ("name", "last_w", "readers", "dsem", "dcount", "last_dma", "excl")

    def __init__(self, name, excl=False):
        self.name = name
        self.excl = excl
        self.last_w = None
        self.readers = []
        self.dsem = None
        self.dcount = 0
        self.last_dma = None


class Op:
    __slots__ = ("eng", "fn", "deps", "is_dma", "dbuf", "token", "needs_inc", "idx")

    def __init__(self, eng, fn, is_dma=False, dbuf=None):
        self.eng = eng
        self.fn = fn
        self.deps = set()
        self.is_dma = is_dma
        self.dbuf = dbuf
        self.token = None
        self.needs_inc = False


class Prog:
    def __init__(self, nc):
        self.nc = nc
        self.ops = []
        self.last_on = {e: None for e in ENGS}
        self.dma_since_barrier = []

    def _add(self, op, reads, writes):
        oid = len(self.ops)
        op.idx = oid
        deps = op.deps
        raw = set()
        for b in reads:
            if b.last_w is not None:
                raw.add(b.last_w)
            if b.excl:
                for r_ in b.readers:
                    if self.ops[r_].eng != op.eng:
                        deps.add(r_)
        war = set()
        for b in writes:
            if b.last_w is not None:
                war.add(b.last_w)
            war.update(b.readers)
        pruned = set()
        for d in deps | raw | war:
            o = self.ops[d]
            if o.eng == op.eng and not o.is_dma and not op.is_dma:
                if op.eng not in SAME_ENG_SYNC or d not in raw:
                    continue
            pruned.add(d)
        op.deps = pruned
        self.ops.append(op)
        for b in reads:
            b.readers.append(oid)
        for b in writes:
            b.last_w = oid
            b.readers = []
        self.last_on[op.eng] = oid
        return oid

    dry = False

    def op(self, eng, fn, reads=(), writes=()):
        if self.dry:
            return None
        return self._add(Op(eng, fn), list(reads), list(writes))

    def dma(self, eng, fn, sb_buf, reads=(), writes=()):
        if self.dry:
            return None
        o = Op(eng, fn, is_dma=True, dbuf=sb_buf)
        if sb_buf.last_dma is not None:
            o.deps.add(sb_buf.last_dma)
        oid = self._add(o, list(reads), list(writes))
        sb_buf.last_dma = oid
        self.dma_since_barrier.append(oid)
        return oid

    def barrier(self):
        if self.dry:
            return
        lasts = [v for v in self.last_on.values() if v is not None]
        dmas = list(self.dma_since_barrier)
        self.dma_since_barrier = []
        for e in ENGS:
            o = Op(e, None)
            o.deps = set(d for d in lasts + dmas if self.ops[d].eng != e or self.ops[d].is_dma)
            o.idx = len(self.ops)
            self.ops.append(o)

    def emit(self, final_wait_ops=()):
        nc = self.nc
        ops = self.ops
        for o in ops:
            for d in o.deps:
                ops[d].needs_inc = True
        for d in final_wait_ops:
            ops[d].needs_inc = True
        esem = {e: nc.alloc_semaphore(name=f"es_{e}") for e in ENGS}
        ndb = 0
        for o in ops:
            if o.is_dma and o.dbuf.dsem is None:
                o.dbuf.dsem = nc.alloc_semaphore(name=f"ds_{ndb}")
                ndb += 1
        cnt = {e: 0 for e in ENGS}
        for o in ops:
            if o.fn is None:
                continue
            if o.is_dma:
                o.dbuf.dcount += 16
                o.token = (o.dbuf.dsem, o.dbuf.dcount)
            elif o.needs_inc:
                cnt[o.eng] += 1
                o.token = (esem[o.eng], cnt[o.eng])
        self.max_cnt = dict(cnt)
        self.n_dsem = ndb
        waited = {e: {} for e in ENGS}
        streams = {e: [] for e in ENGS}
        for o in ops:
            need = {}
            for d in o.deps:
                sem, val = ops[d].token
                k = id(sem)
                if need.get(k, (None, 0))[1] < val:
                    need[k] = (sem, val)
            w = []
            wd = waited[o.eng]
            for k, (sem, val) in need.items():
                if wd.get(k, 0) >= val:
                    continue
                wd[k] = val
                w.append((sem, val))
            if w or o.fn is not None:
                streams[o.eng].append((w, o))
        finals = [ops[d].token for d in final_wait_ops]

        def run(eng_name, e):
            for w, o in streams[eng_name]:
                for sem, val in w:
                    e.wait_ge(sem, val)
                if o.fn is None:
                    continue
                ins = o.fn(e)
                if o.is_dma:
                    ins.then_inc(o.token[0], 16)
                elif o.needs_inc:
                    ins.then_inc(o.token[0], 1)
            if eng_name == "sp":
                for sem, val in finals:
                    e.wait_ge(sem, val)

        with nc.Block() as block:
            @block.tensor
            def _(e):
                run("pe", e)

            @block.scalar
            def _(e):
                run("act", e)

            @block.vector
            def _(e):
                run("dve", e)

            @block.gpsimd
            def _(e):
                run("pool", e)

            @block.sync
            def _(e):
                run("sp", e)


class Arena:
    def __init__(self, ap_bf16, nelem):
        self.ap = ap_bf16
        self.n = nelem
        self.off = 0
        self.gen = 0

    def reset(self):
        self.off = 0
        self.gen += 1

    def bf(self, n, name="a"):
        o = self.off
        self.off += (n + 15) // 16 * 16
        assert self.off <= self.n, (self.off, self.n)
        return self.ap[:, o:o + n], Buf(f"{name}{self.gen}")

    def f32(self, n, name="a"):
        o = self.off
        self.off += 2 * n
        assert self.off <= self.n, (self.off, self.n)
        return self.ap[:, o:o + 2 * n].bitcast(F32), Buf(f"{name}{self.gen}")


def build(n_layers=L, debug=False, stop_after=None, sub=None):
    nc = bass.Bass("TRN2", target_bir_lowering=False)
    P = Prog(nc)

    def din(name, shape):
        return nc.dram_tensor(name, shape, F32, kind="ExternalInput").ap()

    xT = din("xT", [D, S])
    pT = din("pT", [L, 256, S])
    whead = din("whead", [L, 16, 3, 128, 2048])
    wfd = din("wf", [L, D, 4])
    w_gate = din("w_gate", [L, 32, 128, 2048])
    w_br_a = din("w_br_a", [L, 16, 128, 512])
    w_br_b = din("w_br_b", [L, 16, 128, 512])
    w_o = din("w_o", [L, 16, 128, 2048])
    w_up = din("w_up", [L, 64, 128, 2048])
    w_down = din("w_down", [L, 4, 16, 128, 2048])
    w_ple = din("w_ple", [L, 16, 128, 256])
    w_pg = din("w_pg", [L, 16, 128, 2048])
    svd = din("smallvec", [128, 336])
    bfd = din("bf_bc", [128, L * 64])
    cFd = din("constF", [128, 384])
    cBd = din("constB", [128, 544])
    roped = din("ropeCS", [32, 2 * S])
    outT = nc.dram_tensor("outT", [D, S], F32, kind="ExternalOutput").ap()
    hT = nc.dram_tensor("hT", [D, S], F32, kind="ExternalOutput" if debug else "Internal").ap()
    yab = nc.dram_tensor("yab", [1024, S], BF16, kind="ExternalOutput" if debug else "Internal").ap()
    hTB = [[Buf(f"hT{c}_{t}") for t in range(NT)] for c in range(NCH)]
    yabB = [Buf(f"yab{j}") for j in range(8)]

    def sb(name, shape, dt):
        return nc.alloc_sbuf_tensor(name, shape, dt).ap()

    cF = sb("cF", [128, 384], F32); cFB = Buf("cF")
    cB = sb("cB", [128, 544], BF16); cBB = Buf("cB")
    sv = sb("sv", [128, 336], F32); svB = Buf("sv")
    bfb = sb("bfb", [128, L * 64], F32); bfbB = Buf("bfb")
    XT = sb("XT", [128, NCH, S], BF16)
    XB = [Buf(f"X{t}") for t in range(NT)]
    NWS, NWB, NHU = 3, 4, 8
    wst = [sb(f"wst{i}", [128, 16, 128], F32) for i in range(NWS)]
    wstB = [Buf(f"wst{i}") for i in range(NWS)]
    wbf = [sb(f"wbf{i}", [128, 16, 128], BF16) for i in range(NWB)]
    wbfB = [Buf(f"wbf{i}") for i in range(NWB)]
    hu = [sb(f"hu{i}", [128, TT], F32) for i in range(NHU)]
    huB = [Buf(f"hu{i}") for i in range(NHU)]
    rem = nc.sbuf_bytes_remaining
    an = (rem - 1024) // 2 // 32 * 32
    arena_ap = sb("arena", [128, an], BF16)
    A = Arena(arena_ap, an)

    psf = [nc.alloc_psum_tensor(f"ps{i}", [128, 512], F32).ap() for i in range(7)]
    psB = [Buf(f"ps{i}", excl=True) for i in range(7)]
    pstr = nc.alloc_psum_tensor("pstr", [128, 1024], BF16).ap()
    pstrB = Buf("pstr", excl=True)
    st = {}

    def st_reset():
        st.update({"ps": 0, "w": 0, "wl": 0, "wc": 0, "hu": 0, "first_resid_done": False, "nrot": 5})
    st_reset()

    def bank():
        i = st["ps"] % st["nrot"]
        st["ps"] += 1
        return psf[i], psB[i]

    ident = cB[:, 0:128]
    ones_b = cB[:, 128:256]
    mask2 = cB[:, 256:512]
    perm32 = cB[0:32, 512:544]
    perm128 = cB[:, 512:544]
    triU = cF[:, 0:128]
    ones_f = cF[:, 128:256]
    sel127 = cF[:, 256:384]

    P.dma("sp", lambda e: e.dma_start(out=cF, in_=cFd), cFB, writes=[cFB])
    P.dma("sp", lambda e: e.dma_start(out=sv, in_=svd), svB, writes=[svB])
    P.dma("sp", lambda e: e.dma_start(out=bfb, in_=bfd), bfbB, writes=[bfbB])
    A.reset()
    tmpc, tmpcB = A.f32(544, "tmpc")
    P.dma("sp", lambda e: e.dma_start(out=tmpc, in_=cBd), tmpcB, writes=[tmpcB])
    P.op("dve", lambda e: e.tensor_copy(cB, tmpc), reads=[tmpcB], writes=[cBB])
    P.barrier()

    wsched = []
    LA_LOAD, LA_CAST = 3, 2

    def _w_load(i):
        src, kc = wsched[i]
        s_t, s_b = wst[i % NWS], wstB[i % NWS]
        dstv = s_t[:, 0:kc, :].rearrange("p c n -> p (c n)")
        P.dma("sp", lambda e: e.dma_start(out=dstv, in_=src), s_b, writes=[s_b])

    def _w_cast(i):
        src, kc = wsched[i]
        s_t, s_b = wst[i % NWS], wstB[i % NWS]
        w_t, w_b = wbf[i % NWB], wbfB[i % NWB]
        P.op("pool", lambda e: e.tensor_copy(w_t[:, 0:kc, :], s_t[:, 0:kc, :]), reads=[s_b], writes=[w_b])

    def load_w(src, kc):
        i = st["w"]
        st["w"] += 1
        if P.dry:
            wsched.append((src, kc))
            return wbf[i % NWB], wbfB[i % NWB]
        n = len(wsched)
        while st["wl"] < min(n, i + 1 + LA_LOAD):
            while st["wl"] - st["wc"] >= NWS:
                _w_cast(st["wc"])
                st["wc"] += 1
            _w_load(st["wl"])
            st["wl"] += 1
        while st["wc"] < min(n, i + 1 + LA_CAST):
            _w_cast(st["wc"])
            st["wc"] += 1
        return wbf[i % NWB], wbfB[i % NWB]

    def mmgroup(ps, psb, pairs, reads):
        n = len(pairs)
        for i, (l_, r_) in enumerate(pairs):
            P.op("pe", lambda e, l_=l_, r_=r_, i=i: e.matmul(ps, l_, r_, start=(i == 0), stop=(i == n - 1)),
                 reads=reads, writes=[psb])

    def hsrc_ap():
        return hT if st["first_resid_done"] else xT

    pend_hu = {}

    def resid_prefetch(c):
        for tt in range(NT):
            i = st["hu"] % NHU
            st["hu"] += 1
            h_t, h_b = hu[i], huB[i]
            src = hsrc_ap()[c * 128:(c + 1) * 128, tt * TT:(tt + 1) * TT]
            hb = hTB[c][tt]
            P.dma("sp", lambda e, h_t=h_t, src=src: e.dma_start(out=h_t, in_=src), h_b, reads=[hb], writes=[h_b])
            pend_hu[(c, tt)] = (h_t, h_b)

    def resid(c, tt, delta, dbufs):
        h_t, h_b = pend_hu.pop((c, tt))
        dst = hT[c * 128:(c + 1) * 128, tt * TT:(tt + 1) * TT]
        hb = hTB[c][tt]
        P.op("dve", lambda e: e.tensor_tensor(h_t, delta, h_t, ALU.add), reads=[h_b] + dbufs, writes=[h_b])
        return P.dma("sp", lambda e: e.dma_start(out=dst, in_=h_t), h_b, reads=[h_b], writes=[hb])

    def tsl(tt):
        return slice(tt * TT, (tt + 1) * TT)

    def norm_phase(gcol0, final=False):
        A.reset()
        SUB = 256
        NS = S // SUB
        hld = [A.f32(NCH * SUB, "hld") for _ in range(4)]
        sq = [A.bf(SUB, "sq") for _ in range(4)]
        rsl = [A.f32(SUB, "rs") for _ in range(2)]
        outs = []
        if final:
            ot = [A.f32(SUB, "ot") for _ in range(4)]
        src = hsrc_ap().rearrange("(c p) t -> p c t", p=128)
        for s_ in range(NS):
            tt = s_ // 2
            ts_ = slice(s_ * SUB, (s_ + 1) * SUB)
            h_t, h_b = hld[s_ % 4]
            rs, rsB = rsl[s_ % 2]
            h3 = h_t.rearrange("p (c t) -> p c t", c=NCH)
            P.dma("sp" if s_ % 2 == 0 else "act", lambda e, h3=h3, ts_=ts_: e.dma_start(out=h3, in_=src[:, :, ts_]), h_b,
                  reads=[hTB[c][tt] for c in range(NCH)], writes=[h_b])
            ps, psb = bank()
            for c in range(NCH):
                q_t, q_b = sq[c % 4]
                P.op("act", lambda e, q_t=q_t, c=c, h3=h3: e.activation(q_t, h3[:, c, :], AF.Square),
                     reads=[h_b], writes=[q_b])
                P.op("pe", lambda e, q_t=q_t, c=c, ps=ps: e.matmul(ps[:, 0:SUB], ones_b, q_t, start=(c == 0), stop=(c == NCH - 1)),
                     reads=[q_b, cBB], writes=[psb])
            P.op("dve", lambda e, ps=ps, rs=rs: e.tensor_scalar(rs, ps[:, 0:SUB], 1.0 / D, EPS, ALU.mult, ALU.add), reads=[psb], writes=[rsB])
            P.op("act", lambda e, rs=rs: e.activation(rs, rs, AF.Sqrt), reads=[rsB], writes=[rsB])
            P.op("dve", lambda e, rs=rs: e.reciprocal(rs, rs), reads=[rsB], writes=[rsB])
            for c in range(NCH):
                gcol = sv[:, gcol0 + c:gcol0 + c + 1]
                if not final:
                    P.op("dve", lambda e, c=c, h3=h3, gcol=gcol, ts_=ts_, rs=rs: e.scalar_tensor_tensor(
                        XT[:, c, ts_], h3[:, c, :], gcol, rs, ALU.mult, ALU.mult),
                        reads=[h_b, rsB, svB], writes=[XB[tt]])
                else:
                    o_t, o_b = ot[c % 4]
                    P.op("dve", lambda e, c=c, h3=h3, gcol=gcol, o_t=o_t, rs=rs: e.scalar_tensor_tensor(
                        o_t, h3[:, c, :], gcol, rs, ALU.mult, ALU.mult),
                        reads=[h_b, rsB, svB], writes=[o_b])
                    outs.append(P.dma("sp", lambda e, c=c, ts_=ts_, o_t=o_t: e.dma_start(
                        out=outT[c * 128:(c + 1) * 128, ts_], in_=o_t), o_b, reads=[o_b]))
        P.barrier()
        return outs

    def mixer(l):
        A.reset()
        qc, _ = A.bf(S, "qc")
        kc, _ = A.bf(S, "kc")
        vc, _ = A.bf(S, "vc")
        Vt, _ = A.bf(S, "V")
        qcB = [Buf(f"qc{t_}") for t_ in range(NT)]
        kcB = [Buf(f"kc{t_}") for t_ in range(NT)]
        vcB = [Buf(f"vc{t_}") for t_ in range(NT)]
        VB = [Buf(f"V{t_}") for t_ in range(4)]
        V3 = Vt.rearrange("p (t d) -> p t d", d=128)
        nd, ndB = A.f32(2 * S, "nd")
        nd3 = nd.rearrange("p (a t) -> p a t", a=2)
        rope, ropeB = A.f32(2 * S, "rope")
        ropeC = rope[0:32, 0:S]
        ropeS = rope[0:32, S:2 * S]
        yat = [A.bf(S, "yat") for _ in range(2)]
        ptd = [A.bf(256, "ptd") for _ in range(8)]
        ptf = [A.bf(512, "ptf") for _ in range(6)]
        t1, t1B = A.f32(TT, "t1")
        t2, t2B = A.f32(TT, "t2")
        qn2 = [A.bf(TT, "qn") for _ in range(2)]
        rd, rdB = A.f32(TT, "rd")
        biasT, biasB = A.f32(256, "biasT")
        fb, fbB = A.f32(64, "fb")
        nls, nlsB = A.f32(64, "nls")
        lsx, lsxB = A.f32(64, "lsx")
        ncs, ncsB = A.f32(64, "ncs")
        ncref, ncrefB = A.f32(64, "ncref")
        wfs, wfsB = A.f32(64, "wfs")
        wfb, wfbB = A.bf(64, "wfb")
        wfs3 = wfs.rearrange("p (c n) -> p c n", n=4)
        wfb3 = wfb.rearrange("p (c n) -> p c n", n=4)

        P.dma("sp", lambda e: e.dma_start(out=rope[0:32, :], in_=roped), ropeB, writes=[ropeB])
        P.dma("sp", lambda e: e.dma_start(out=wfs3, in_=wfd[l].rearrange("(c p) n -> p c n", p=128)), wfsB, writes=[wfsB])
        P.op("dve", lambda e: e.tensor_copy(wfb, wfs), reads=[wfsB], writes=[wfbB])
        psF, psFB = bank()
        for t16 in range(16):
            for k in range(NCH):
                P.op("pe", lambda e, t16=t16, k=k: e.matmul(
                    psF[:, t16 * 4:(t16 + 1) * 4], XT[:, k, t16 * 128:(t16 + 1) * 128], wfb3[:, k, :],
                    start=(k == 0), stop=(k == NCH - 1)), reads=[XB[t16 // 4], wfbB], writes=[psFB])
        P.op("dve", lambda e: e.tensor_tensor(fb, psF[:, 0:64], bfb[:, l * 64:(l + 1) * 64], ALU.add),
             reads=[psFB, bfbB], writes=[fbB])
        P.op("act", lambda e: e.activation(fb, fb, AF.Exp, scale=-1.0), reads=[fbB], writes=[fbB])
        P.op("dve", lambda e: e.tensor_scalar_add(fb, fb, 1.0), reads=[fbB], writes=[fbB])
        P.op("act", lambda e: e.activation(nls, fb, AF.Ln), reads=[fbB], writes=[nlsB])
        P.op("dve", lambda e: e.memset(lsx[:, 0:4], 0.0), writes=[lsxB])
        for t in range(1, 16):
            P.op("dve", lambda e, t=t: e.tensor_tensor(lsx[:, t * 4:(t + 1) * 4], lsx[:, (t - 1) * 4:t * 4],
                                                       nls[:, (t - 1) * 4:t * 4], ALU.add),
                 reads=[lsxB, nlsB], writes=[lsxB])
        psC, psCB = bank()
        P.op("pe", lambda e: e.matmul(psC[:, 0:64], triU, nls, start=True, stop=False), reads=[nlsB, cFB], writes=[psCB])
        P.op("pe", lambda e: e.matmul(psC[:, 0:64], ones_f, lsx, start=False, stop=True), reads=[lsxB, cFB], writes=[psCB])
        P.op("dve", lambda e: e.tensor_copy(ncs, psC[:, 0:64]), reads=[psCB], writes=[ncsB])
        psR, psRB = bank()
        P.op("pe", lambda e: e.matmul(psR[:, 0:64], sel127, ncs, start=True, stop=True), reads=[ncsB, cFB], writes=[psRB])
        P.op("dve", lambda e: e.tensor_copy(ncref, psR[:, 0:64]), reads=[psRB], writes=[ncrefB])
        if sub == "prelude":
            P.barrier()
            return

        def tts_of(d, lo, hi):
            Lc = S // d
            r, m0 = divmod(lo, Lc)
            m1 = m0 + (hi - lo)
            return list(range((m0 * d) // TT, min(NT - 1, ((m1 - 1) * d + r) // TT) + 1))

        def project(hd, d, rope_on):
            pend = []
            gi = 0
            trq = []
            for pi, (dst, dstBl) in ((2, (vc, vcB)), (1, (kc, kcB)), (0, (qc, qcB))):
                w_t, w_b = load_w(whead[l, hd, pi], NCH)
                d3 = dst.rearrange("p (r m) -> p r m", r=d)
                for tt in range(NT):
                    ps, psb = bank()
                    mmgroup(ps, psb, [(w_t[:, k, :], XT[:, k, tsl(tt)]) for k in range(NCH)], [w_b, XB[tt]])
                    n = TT // d
                    dv = d3[:, :, tt * n:(tt + 1) * n]
                    sv_ = ps.rearrange("p (m r) -> p r m", r=d)
                    P.op("act", lambda e, dv=dv, sv_=sv_: e.activation(dv, sv_, AF.Copy), reads=[psb], writes=[dstBl[tt]])
                    tail = None
                    if rope_on and pi < 2:
                        q_t, q_b = qn2[gi % 2]
                        gi += 1
                        P.op("act", lambda e, ps=ps, q_t=q_t: e.activation(q_t, ps, AF.Copy), reads=[psb], writes=[q_b])

                        def tail(ps=ps, psb=psb, q_t=q_t, q_b=q_b, tt=tt, d3=d3, dstB=dstBl[tt], n=n):
                            pw, pwb = bank()
                            P.op("pe", lambda e: e.matmul(pw[0:32, :], perm128, q_t, start=True, stop=True),
                                 reads=[q_b, cBB], writes=[pwb])
                            P.op("dve", lambda e: e.tensor_tensor(t1[0:32, :], ps[0:32, :], ropeC[:, tsl(tt)], ALU.mult),
                                 reads=[psb, ropeB], writes=[t1B])
                            P.op("dve", lambda e: e.tensor_tensor(t2[0:32, :], pw[0:32, :], ropeS[:, tsl(tt)], ALU.mult),
                                 reads=[pwb, ropeB], writes=[t2B])
                            t1v = t1[0:32, :].rearrange("p (m r) -> p r m", r=d)
                            t2v = t2[0:32, :].rearrange("p (m r) -> p r m", r=d)
                            dv32 = d3[0:32, :, tt * n:(tt + 1) * n]
                            P.op("dve", lambda e: e.tensor_tensor(dv32, t1v, t2v, ALU.add),
                                 reads=[t1B, t2B], writes=[dstB])
                    for f_ in pend:
                        f_()
                    pend = [tail] if tail is not None else []
                    if pi == 2 and tt == NT - 1:
                        trq = [0, 1, 2, 3]
                    elif pi == 1 and trq:
                        i4 = trq.pop(0)
                        half = (i4 % 2) * 512
                        for ii in range(4):
                            i = i4 * 4 + ii
                            P.op("pe", lambda e, i=i, ii=ii, half=half: e.transpose(
                                pstr[:, half + ii * 128:half + (ii + 1) * 128], vc[:, i * 128:(i + 1) * 128], ident),
                                reads=[vcB[t_] for t_ in tts_of(d, i * 128, (i + 1) * 128)] + [cBB], writes=[pstrB])
                        P.op("act", lambda e, i4=i4, half=half: e.activation(
                            V3[:, i4 * 4:(i4 + 1) * 4, :], pstr[:, half:half + 512].rearrange("p (t d) -> p t d", d=128), AF.Copy),
                            reads=[pstrB], writes=[VB[i4]])
            for f_ in pend:
                f_()

        for j in range(4):
            for g in range(3):
                d = DILS[g]
                nb = 16 // d
                project(g * 4 + j, d, True)
                if sub == "proj":
                    P.barrier()
                    return
                nd4 = nd3.rearrange("p a (m r) -> p a r m", r=d)
                pts = {}

                def dil_s1(i, nb=nb, d_=d):
                    r, n = divmod(i, nb)
                    nq = 256 if n < nb - 1 else 128
                    pt, ptB = ptd[i % 8]
                    pts[i] = (pt, ptB)
                    ps, psb = bank()
                    P.op("pe", lambda e: e.matmul(ps[:, 0:nq], kc[:, i * 128:(i + 1) * 128],
                                                  qc[:, i * 128:i * 128 + nq], start=True, stop=True),
                         reads=[kcB[t_] for t_ in tts_of(d_, i * 128, (i + 1) * 128)]
                         + [qcB[t_] for t_ in tts_of(d_, i * 128, i * 128 + nq)], writes=[psb])
                    P.op("act", lambda e: e.activation(pt[:, 0:nq], ps[:, 0:nq], AF.Exp, scale=SCALE),
                         reads=[psb], writes=[ptB])
                    P.op("pool", lambda e: e.tensor_tensor(pt[:, 0:nq], pt[:, 0:nq], mask2[:, 0:nq], ALU.mult),
                         reads=[ptB, cBB], writes=[ptB])

                def dil_s2(i, nb=nb, g=g, nd4=nd4):
                    r, n = divmod(i, nb)
                    pt, ptB = pts[i]
                    po, pob = bank()
                    pairs_o = [(V3[:, i, :], pt[:, 0:128])]
                    pairs_d = [(ones_b, pt[:, 0:128])]
                    rds = [VB[i // 4], ptB, cBB]
                    if n > 0:
                        ppt, pptB = pts[i - 1]
                        pairs_o.append((V3[:, i - 1, :], ppt[:, 128:256]))
                        pairs_d.append((ones_b, ppt[:, 128:256]))
                        rds = rds + [pptB, VB[(i - 1) // 4]]
                    mmgroup(po[:, 0:128], pob, pairs_o, rds)
                    mmgroup(po[:, 128:256], pob, pairs_d, rds)
                    dstv = nd4[:, :, r, n * 128:(n + 1) * 128]
                    srcv = po[:, 0:256].rearrange("p (a q) -> p a q", a=2)
                    if g == 0:
                        P.op("dve", lambda e: e.tensor_copy(dstv, srcv), reads=[pob], writes=[ndB])
                    else:
                        P.op("dve", lambda e: e.tensor_tensor(dstv, srcv, dstv, ALU.add),
                             reads=[pob, ndB], writes=[ndB])

                if 'noattn' not in os.environ.get('KDBG', ''):
                    LAD = 5
                    for i in range(16 + LAD):
                        if i < 16:
                            dil_s1(i)
                        if i >= LAD:
                            dil_s2(i - LAD)
            if sub == "dil":
                P.barrier()
                return
            y_t, y_b = yat[0]
            P.op("dve", lambda e: e.reciprocal(nd3[:, 1, :], nd3[:, 1, :]), reads=[ndB], writes=[ndB])
            P.op("dve", lambda e, y_t=y_t: e.tensor_tensor(y_t, nd3[:, 0, :], nd3[:, 1, :], ALU.mult), reads=[ndB], writes=[y_b])
            P.dma("sp", lambda e, y_t=y_t, j=j: e.dma_start(out=yab[j * 128:(j + 1) * 128, :], in_=y_t), y_b,
                  reads=[y_b], writes=[yabB[j]])

            if sub == "dilout":
                P.barrier()
                return
            project(12 + j, 1, False)
            for kb in range(16):
                P.op("dve", lambda e, kb=kb, j=j: e.tensor_scalar(
                    biasT[:, kb * 16:(kb + 1) * 16], ncref[:, j:64:4], ncs[:, kb * 4 + j:kb * 4 + j + 1], -1.0,
                    ALU.subtract, ALU.mult), reads=[ncrefB, ncsB], writes=[biasB])
            y_t, y_b = yat[1]
            its = [(ch, kb) for ch in range(4) for kb in range(4 * ch + 4)]
            fpt = {}

            def fox_s1(t, j=j):
                ch, kb = its[t]
                irel = max(0, kb - 4 * ch)
                qlo = irel * 128
                nq = 512 - qlo
                pt, ptB = ptf[t % 6]
                fpt[t] = (pt, ptB)
                ps, psb = bank()
                P.op("pe", lambda e: e.matmul(
                    ps[:, 0:nq], kc[:, kb * 128:(kb + 1) * 128], qc[:, ch * 512 + qlo:(ch + 1) * 512],
                    start=True, stop=True), reads=[kcB[kb // 4], qcB[ch]], writes=[psb])
                P.op("act", lambda e: e.activation(
                    pt[:, 0:nq], ps[:, 0:nq], AF.Exp,
                    bias=biasT[:, kb * 16 + 4 * ch + 1:kb * 16 + 4 * ch + 2], scale=SCALE),
                    reads=[psb, biasB], writes=[ptB])
                if kb >= 4 * ch:
                    P.op("pool", lambda e: e.tensor_tensor(pt[:, 0:128], pt[:, 0:128], mask2[:, 0:128], ALU.mult),
                         reads=[ptB, cBB], writes=[ptB])

            def fox_s2(t, y_t=y_t, y_b=y_b):
                ch, kb = its[t]
                nkb = 4 * ch + 4
                irel = max(0, kb - 4 * ch)
                qlo = irel * 128
                nq = 512 - qlo
                pt, ptB = fpt[t]
                po, pob = (psf[5], psB[5]) if ch % 2 == 0 else (psf[3], psB[3])
                pd, pdb = (psf[6], psB[6]) if ch % 2 == 0 else (psf[4], psB[4])
                first, last = (kb == 0), (kb == nkb - 1)
                P.op("pe", lambda e: e.matmul(po[:, qlo:512], V3[:, kb, :], pt[:, 0:nq], start=first, stop=last),
                     reads=[VB[kb // 4], ptB], writes=[pob])
                P.op("pe", lambda e: e.matmul(pd[:, qlo:512], ones_b, pt[:, 0:nq], start=first, stop=last),
                     reads=[cBB, ptB], writes=[pdb])
                if last:
                    P.op("dve", lambda e: e.reciprocal(rd, pd), reads=[pdb], writes=[rdB])
                    P.op("dve", lambda e: e.tensor_tensor(y_t[:, ch * 512:(ch + 1) * 512], po, rd, ALU.mult),
                         reads=[pob, rdB], writes=[y_b])

            LA = 4
            st["nrot"] = 3
            for t in range(len(its) + LA if 'noattn' not in os.environ.get('KDBG', '') else 0):
                if t < len(its):
                    fox_s1(t)
                if t >= LA:
                    fox_s2(t - LA)
            st["nrot"] = 5
            P.dma("sp", lambda e, y_t=y_t, j=j: e.dma_start(out=yab[512 + j * 128:512 + (j + 1) * 128, :], in_=y_t), y_b,
                  reads=[y_b], writes=[yabB[4 + j]])
        P.barrier()

    def merge_phase(l):
        A.reset()
        mg, mgB = A.bf(8 * S, "mg")
        mg3 = mg.rearrange("p (c t) -> p c t", c=8)
        yh, yhB = A.bf(8 * S, "yh")
        yh3 = yh.rearrange("p (c t) -> p c t", c=8)
        ga = [A.f32(TT, "ga") for _ in range(NT)]
        gb = [A.f32(TT, "gb") for _ in range(NT)]
        bg0 = 80 * l + 48
        P.dma("sp", lambda e: e.dma_start(out=yh3, in_=yab.rearrange("(c p) t -> p c t", p=128)),
              yhB, reads=yabB, writes=[yhB])
        for hh in range(2):
            for n8 in range(8):
                n = hh * 8 + n8
                wga, wgab = load_w(w_gate[l, n], NCH)
                for tt in range(NT):
                    pga, pgab = bank()
                    mmgroup(pga, pgab, [(wga[:, k, :], XT[:, k, tsl(tt)]) for k in range(NCH)], [wgab, XB[tt]])
                    ga_t, ga_b = ga[tt]
                    P.op("act", lambda e, ga_t=ga_t, pga=pga, n=n: e.activation(
                        ga_t, pga, AF.Sigmoid, bias=sv[:, bg0 + n:bg0 + n + 1]), reads=[pgab, svB], writes=[ga_b])
                wgb, wgbb = load_w(w_gate[l, 16 + n], NCH)
                for tt in range(NT):
                    pgb, pgbb = bank()
                    mmgroup(pgb, pgbb, [(wgb[:, k, :], XT[:, k, tsl(tt)]) for k in range(NCH)], [wgbb, XB[tt]])
                    gb_t, gb_b = gb[tt]
                    P.op("act", lambda e, gb_t=gb_t, pgb=pgb, n=n: e.activation(
                        gb_t, pgb, AF.Sigmoid, bias=sv[:, bg0 + 16 + n:bg0 + 16 + n + 1]), reads=[pgbb, svB], writes=[gb_b])
                wa, wab = load_w(w_br_a[l, n], 4)
                for tt in range(NT):
                    pa, pab = bank()
                    mmgroup(pa, pab, [(wa[:, jj, :], yh3[:, jj, tsl(tt)]) for jj in range(4)], [wab, yhB])
                    ga_t, ga_b = ga[tt]
                    P.op("dve", lambda e, pa=pa, ga_t=ga_t: e.tensor_tensor(ga_t, pa, ga_t, ALU.mult),
                         reads=[pab, ga_b], writes=[ga_b])
                wb_, wbb = load_w(w_br_b[l, n], 4)
                for tt in range(NT):
                    pb, pbb = bank()
                    mmgroup(pb, pbb, [(wb_[:, jj, :], yh3[:, 4 + jj, tsl(tt)]) for jj in range(4)], [wbb, yhB])
                    ga_t, ga_b = ga[tt]
                    gb_t, gb_b = gb[tt]
                    P.op("dve", lambda e, pb=pb, gb_t=gb_t: e.tensor_tensor(gb_t, pb, gb_t, ALU.mult),
                         reads=[pbb, gb_b], writes=[gb_b])
                    P.op("pool", lambda e, n8=n8, tt=tt, ga_t=ga_t, gb_t=gb_t: e.tensor_tensor(mg3[:, n8, tsl(tt)], ga_t, gb_t, ALU.add),
                         reads=[ga_b, gb_b], writes=[mgB])
            resid_prefetch(0)
            for c in range(NCH):
                wo, wob = load_w(w_o[l, c, :, hh * 1024:(hh + 1) * 1024], 8)
                if c + 1 < NCH:
                    resid_prefetch(c + 1)
                for tt in range(NT):
                    ps, psb = bank()
                    mmgroup(ps, psb, [(wo[:, n8, :], mg3[:, n8, tsl(tt)]) for n8 in range(8)], [wob, mgB])
                    resid(c, tt, ps, [psb])
            if hh == 0:
                st["first_resid_done"] = True
        P.barrier()

    def mlp_phase(l):
        A.reset()
        at, atB = A.bf(NCH * S, "at")
        at3 = at.rearrange("p (f t) -> p f t", f=NCH)
        rl = [A.f32(TT, "rl") for _ in range(2)]
        it = 0
        for g in range(4):
            for f in range(NCH):
                fc = g * NCH + f
                wu, wub = load_w(w_up[l, fc], NCH)
                for tt in range(NT):
                    ps, psb = bank()
                    mmgroup(ps, psb, [(wu[:, k, :], XT[:, k, tsl(tt)]) for k in range(NCH)], [wub, XB[tt]])
                    r_t, r_b = rl[it % 2]
                    it += 1
                    P.op("act", lambda e, r_t=r_t, ps=ps: e.activation(r_t, ps, AF.Relu), reads=[psb], writes=[r_b])
                    P.op("dve", lambda e, r_t=r_t, f=f, tt=tt: e.tensor_tensor(at3[:, f, tsl(tt)], r_t, r_t, ALU.mult),
                         reads=[r_b], writes=[atB])
            resid_prefetch(0)
            for c in range(NCH):
                wd, wdb = load_w(w_down[l, g, c], NCH)
                if c + 1 < NCH:
                    resid_prefetch(c + 1)
                for tt in range(NT):
                    ps, psb = bank()
                    mmgroup(ps, psb, [(wd[:, f, :], at3[:, f, tsl(tt)]) for f in range(NCH)], [wdb, atB])
                    resid(c, tt, ps, [psb])
        P.barrier()

    def ple_phase(l):
        A.reset()
        pst_, pstB_ = A.f32(2 * S, "pst")
        pb_, pbB_ = A.bf(2 * S, "pb")
        pst3 = pst_.rearrange("p (c t) -> p c t", c=2)
        pb3 = pb_.rearrange("p (c t) -> p c t", c=2)
        gt = [A.f32(TT, "gt") for _ in range(2)]
        P.dma("sp", lambda e: e.dma_start(out=pst3, in_=pT[l].rearrange("(c p) t -> p c t", p=128)), pstB_, writes=[pstB_])
        P.op("dve", lambda e: e.tensor_copy(pb_, pst_), reads=[pstB_], writes=[pbB_])
        it = 0
        resid_prefetch(0)
        for c in range(NCH):
            if c + 1 < NCH:
                resid_prefetch(c + 1)
            wg, wgb_ = load_w(w_pg[l, c], NCH)
            wp, wpb = load_w(w_ple[l, c], 2)
            for tt in range(NT):
                pg, pgb_ = bank()
                mmgroup(pg, pgb_, [(wg[:, k, :], XT[:, k, tsl(tt)]) for k in range(NCH)], [wgb_, XB[tt]])
                pp, ppb = bank()
                mmgroup(pp, ppb, [(wp[:, k, :], pb3[:, k, tsl(tt)]) for k in range(2)], [wpb, pbB_])
                g_t, g_b = gt[it % 2]
                it += 1
                P.op("act", lambda e, g_t=g_t, pg=pg: e.activation(g_t, pg, AF.Sigmoid), reads=[pgb_], writes=[g_b])
                P.op("dve", lambda e, g_t=g_t, pp=pp: e.tensor_tensor(g_t, pp, g_t, ALU.mult), reads=[ppb, g_b], writes=[g_b])
                resid(c, tt, g_t, [g_b])
        P.barrier()

    def run_all():
        for l in range(n_layers):
            norm_phase(80 * l + 0)
            if stop_after == "norm":
                break
            mixer(l)
            if stop_after == "mixer":
                break
            merge_phase(l)
            if stop_after == "merge":
                break
            norm_phase(80 * l + 16)
            mlp_phase(l)
            if stop_after == "mlp":
                break
            norm_phase(80 * l + 32)
            ple_phase(l)
        return norm_phase(320, final=True)

    P.dry = True
    run_all()
    P.dry = False
    st_reset()
    finals = run_all()
    P.emit(final_wait_ops=finals)
    return nc, P


def _consts():
    k = np.arange(128)
    triU = (k[:, None] <= k[None, :]).astype(np.float32)
    ones = np.ones((128, 128), np.float32)
    sel = np.zeros((128, 128), np.float32)
    sel[127, :] = 1.0
    constF = np.concatenate([triU, ones, sel], axis=1)
    ident = np.eye(128, dtype=np.float32)
    m_cur = (k[:, None] <= k[None, :]).astype(np.float32)
    m_prev = (k[:, None] >= k[None, :]).astype(np.float32)
    perm = np.zeros((128, 32), np.float32)
    for m in range(32):
        perm[(m + 16) % 32, m] = 1.0
    constB = np.concatenate([ident, ones, m_cur, m_prev, perm], axis=1)
    half = 16
    inv = (500000.0 ** (-np.arange(half, dtype=np.float32) / half)).astype(np.float32)
    ang = (np.arange(S, dtype=np.float32)[:, None] * inv[None, :]).astype(np.float32)
    cos = np.cos(ang).astype(np.float32).T
    sin = np.sin(ang).astype(np.float32).T
    C = np.concatenate([cos, cos], axis=0)
    Sn = np.concatenate([-sin, sin], axis=0)
    rope = np.concatenate([C, Sn], axis=1).astype(np.float32)
    return constF, constB, rope


_CACHE = {}


def _prep_shared(inp):
    f = np.float32
    w_in = np.asarray(inp["w_in"], f)
    def tiled(w, kc):
        n = w.shape[2] // 128
        return np.ascontiguousarray(np.asarray(w, f).reshape(L, kc, 128, n, 128).transpose(0, 3, 2, 1, 4).reshape(L, n, 128, kc * 128))
    whead = np.ascontiguousarray(
        w_in[:, :, :6144].reshape(L, 16, 128, 3, 16, 128).transpose(0, 4, 3, 2, 1, 5).reshape(L, 16, 3, 128, 2048))
    wf = np.ascontiguousarray(w_in[:, :, 6144:6148])
    sv = np.zeros((128, 336), f)
    for l in range(L):
        sv[:, 80 * l + 0:80 * l + 16] = np.asarray(inp["g_mix"][l], f).reshape(16, 128).T
        sv[:, 80 * l + 16:80 * l + 32] = np.asarray(inp["g_mlp"][l], f).reshape(16, 128).T
        sv[:, 80 * l + 32:80 * l + 48] = np.asarray(inp["g_ple"][l], f).reshape(16, 128).T
        sv[:, 80 * l + 48:80 * l + 80] = np.asarray(inp["b_gate"][l], f).reshape(32, 128).T
    sv[:, 320:336] = np.asarray(inp["g_final"], f).reshape(16, 128).T
    bf = np.asarray(inp["b_f"], f)
    bf_bc = np.ascontiguousarray(np.broadcast_to(bf[None, :, None, :], (128, L, 16, 4)).reshape(128, L * 64))
    constF, constB, rope = _consts()
    shared = {
        "whead": whead, "wf": wf,
        "w_gate": tiled(inp["w_gate"], 16), "w_br_a": tiled(inp["w_br_a"], 4),
        "w_br_b": tiled(inp["w_br_b"], 4), "w_o": tiled(inp["w_o"], 16),
        "w_up": tiled(inp["w_up"], 16),
        "w_down": np.ascontiguousarray(np.asarray(inp["w_down"], f).reshape(L, 4, 16, 128, 16, 128).transpose(0, 1, 4, 3, 2, 5).reshape(L, 4, 16, 128, 2048)),
        "w_ple": tiled(inp["w_ple"], 2), "w_pg": tiled(inp["w_ple_gate"], 16),
        "smallvec": sv, "bf_bc": bf_bc, "constF": constF, "constB": constB, "ropeCS": rope,
    }
    return shared


def kernel(**inputs):
    x = np.asarray(inputs["x"], np.float32)
    p = np.asarray(inputs["p"], np.float32)
    shared = _prep_shared(inputs)
    if "nc" not in _CACHE:
        _CACHE["nc"] = build()[0]
    nc = _CACHE["nc"]
    in_maps = []
    for b in range(8):
        m = dict(shared)
        m["xT"] = np.ascontiguousarray(x[b].T)
        m["pT"] = np.ascontiguousarray(p[:, b].transpose(0, 2, 1))
        in_maps.append(m)
    res = run_bass_kernel_spmd(nc, in_maps, core_ids=list(range(8)))
    out = np.stack([np.ascontiguousarray(res.results[b]["outT"].T) for b in range(8)], axis=0)
    return out.astype(np.float32)
```

```python
import os
import numpy as np
import concourse.bass as bass
import concourse.mybir as mybir
from concourse.bass_utils import run_bass_kernel_spmd

F32 = mybir.dt.float32
BF16 = mybir.dt.bfloat16
AF = mybir.ActivationFunctionType
ALU = mybir.AluOpType

D = 2048
S = 2048
L = 4
NCH = 16
NT = 4
TT = 512
DFF = 8192
EPS = 1e-6
SCALE = 128 ** -0.5
DILS = (1, 4, 16)

ENGS = ("pe", "act", "dve", "pool", "sp")
SAME_ENG_SYNC = {"act", "dve", "pool"}


class Buf:
    __slots__ = ("name", "last_w", "readers", "dsem", "dcount", "last_dma", "excl")

    def __init__(self, name, excl=False):
        self.name = name
        self.excl = excl
        self.last_w = None
        self.readers = []
        self.dsem = None
        self.dcount = 0
        self.last_dma = None


class Op:
    __slots__ = ("eng", "fn", "deps", "is_dma", "dbuf", "token", "needs_inc", "idx")

    def __init__(self, eng, fn, is_dma=False, dbuf=None):
        self.eng = eng
        self.fn = fn
        self.deps = set()
        self.is_dma = is_dma
        self.dbuf = dbuf
        self.token = None
        self.needs_inc = False


class Prog:
    def __init__(self, nc):
        self.nc = nc
        self.ops = []
        self.last_on = {e: None for e in ENGS}
        self.dma_since_barrier = []

    def _add(self, op, reads, writes):
        oid = len(self.ops)
        op.idx = oid
        deps = op.deps
        raw = set()
        for b in reads:
            if b.last_w is not None:
                raw.add(b.last_w)
            if b.excl:
                for r_ in b.readers:
                    if self.ops[r_].eng != op.eng:
                        deps.add(r_)
        war = set()
        for b in writes:
            if b.last_w is not None:
                war.add(b.last_w)
            war.update(b.readers)
        pruned = set()
        for d in deps | raw | war:
            o = self.ops[d]
            if o.eng == op.eng and not o.is_dma and not op.is_dma:
                if op.eng not in SAME_ENG_SYNC or d not in raw:
                    continue
            pruned.add(d)
        op.deps = pruned
        self.ops.append(op)
        for b in reads:
            b.readers.append(oid)
        for b in writes:
            b.last_w = oid
            b.readers = []
        self.last_on[op.eng] = oid
        return oid

    dry = False

    def op(self, eng, fn, reads=(), writes=()):
        if self.dry:
            return None
        return self._add(Op(eng, fn), list(reads), list(writes))

    def dma(self, eng, fn, sb_buf, reads=(), writes=()):
        if self.dry:
            return None
        o = Op(eng, fn, is_dma=True, dbuf=sb_buf)
        if sb_buf.last_dma is not None:
            o.deps.add(sb_buf.last_dma)
        oid = self._add(o, list(reads), list(writes))
        sb_buf.last_dma = oid
        self.dma_since_barrier.append(oid)
        return oid

    def barrier(self):
        if self.dry:
            return
        lasts = [v for v in self.last_on.values() if v is not None]
        dmas = list(self.dma_since_barrier)
        self.dma_since_barrier = []
        for e in ENGS:
            o = Op(e, None)
            o.deps = set(d for d in lasts + dmas if self.ops[d].eng != e or self.ops[d].is_dma)
            o.idx = len(self.ops)
            self.ops.append(o)

    def emit(self, final_wait_ops=()):
        nc = self.nc
        ops = self.ops
        for o in ops:
            for d in o.deps:
                ops[d].needs_inc = True
        for d in final_wait_ops:
            ops[d].needs_inc = True
        esem = {e: nc.alloc_semaphore(name=f"es_{e}") for e in ENGS}
        ndb = 0
        for o in ops:
            if o.is_dma and o.dbuf.dsem is None:
                o.dbuf.dsem = nc.alloc_semaphore(name=f"ds_{ndb}")
                ndb += 1
        cnt = {e: 0 for e in ENGS}
        for o in ops:
            if o.fn is None:
                continue
            if o.is_dma:
                o.dbuf.dcount += 16
                o.token = (o.dbuf.dsem, o.dbuf.dcount)
            elif o.needs_inc:
                cnt[o.eng] += 1
                o.token = (esem[o.eng], cnt[o.eng])
        self.max_cnt = dict(cnt)
        self.n_dsem = ndb
        waited = {e: {} for e in ENGS}
        streams = {e: [] for e in ENGS}
        for o in ops:
            need = {}
            for d in o.deps:
                sem, val = ops[d].token
                k = id(sem)
                if need.get(k, (None, 0))[1] < val:
                    need[k] = (sem, val)
            w = []
            wd = waited[o.eng]
            for k, (sem, val) in need.items():
                if wd.get(k, 0) >= val:
                    continue
                wd[k] = val
                w.append((sem, val))
            if w or o.fn is not None:
                streams[o.eng].append((w, o))
        finals = [ops[d].token for d in final_wait_ops]

        def run(eng_name, e):
            for w, o in streams[eng_name]:
                for sem, val in w:
                    e.wait_ge(sem, val)
                if o.fn is None:
                    continue
                ins = o.fn(e)
                if o.is_dma:
                    ins.then_inc(o.token[0], 16)
                elif o.needs_inc:
                    ins.then_inc(o.token[0], 1)
            if eng_name == "sp":
                for sem, val in finals:
                    e.wait_ge(sem, val)

        with nc.Block() as block:
            @block.tensor
            def _(e):
                run("pe", e)

            @block.scalar
            def _(e):
                run("act", e)

            @block.vector
            def _(e):
                run("dve", e)

            @block.gpsimd
            def _(e):
                run("pool", e)

            @block.sync
            def _(e):
                run("sp", e)


class Arena:
    def __init__(self, ap_bf16, nelem):
        self.ap = ap_bf16
        self.n = nelem
        self.off = 0
        self.gen = 0

    def reset(self):
        self.off = 0
        self.gen += 1

    def bf(self, n, name="a"):
        o = self.off
        self.off += (n + 15) // 16 * 16
        assert self.off <= self.n, (self.off, self.n)
        return self.ap[:, o:o + n], Buf(f"{name}{self.gen}")

    def f32(self, n, name="a"):
        o = self.off
        self.off += 2 * n
        assert self.off <= self.n, (self.off, self.n)
        return self.ap[:, o:o + 2 * n].bitcast(F32), Buf(f"{name}{self.gen}")


def build(n_layers=L, debug=False, stop_after=None, sub=None):
    nc = bass.Bass("TRN2", target_bir_lowering=False)
    P = Prog(nc)

    def din(name, shape):
        return nc.dram_tensor(name, shape, F32, kind="ExternalInput").ap()

    xT = din("xT", [D, S])
    pT = din("pT", [L, 256, S])
    whead = din("whead", [L, 16, 3, 128, 2048])
    wfd = din("wf", [L, D, 4])
    w_gate = din("w_gate", [L, 32, 128, 2048])
    w_br_a = din("w_br_a", [L, 16, 128, 512])
    w_br_b = din("w_br_b", [L, 16, 128, 512])
    w_o = din("w_o", [L, 16, 128, 2048])
    w_up = din("w_up", [L, 64, 128, 2048])
    w_down = din("w_down", [L, 4, 16, 128, 2048])
    w_ple = din("w_ple", [L, 16, 128, 256])
    w_pg = din("w_pg", [L, 16, 128, 2048])
    svd = din("smallvec", [128, 336])
    bfd = din("bf_bc", [128, L * 64])
    cFd = din("constF", [128, 384])
    cBd = din("constB", [128, 800])
    roped = din("ropeCS", [32, 2 * S])
    outT = nc.dram_tensor("outT", [D, S], F32, kind="ExternalOutput").ap()
    hT = nc.dram_tensor("hT", [D, S], F32, kind="ExternalOutput" if debug else "Internal").ap()
    yab = nc.dram_tensor("yab", [1024, S], BF16, kind="ExternalOutput" if debug else "Internal").ap()
    hTB = [[Buf(f"hT{c}_{t}") for t in range(NT)] for c in range(NCH)]
    yabB = [Buf(f"yab{j}") for j in range(8)]

    def sb(name, shape, dt):
        return nc.alloc_sbuf_tensor(name, shape, dt).ap()

    cF = sb("cF", [128, 384], F32); cFB = Buf("cF")
    cB = sb("cB", [128, 800], BF16); cBB = Buf("cB")
    sv = sb("sv", [128, 336], F32); svB = Buf("sv")
    bfb = sb("bfb", [128, L * 64], F32); bfbB = Buf("bfb")
    XT = sb("XT", [128, NCH, S], BF16)
    XB = [Buf(f"X{t}") for t in range(NT)]
    NWS, NWB, NHU = 3, 4, 8
    wst = [sb(f"wst{i}", [128, 16, 128], F32) for i in range(NWS)]
    wstB = [Buf(f"wst{i}") for i in range(NWS)]
    wbf = [sb(f"wbf{i}", [128, 16, 128], BF16) for i in range(NWB)]
    wbfB = [Buf(f"wbf{i}") for i in range(NWB)]
    hu = [sb(f"hu{i}", [128, TT], F32) for i in range(NHU)]
    huB = [Buf(f"hu{i}") for i in range(NHU)]
    rem = nc.sbuf_bytes_remaining
    an = (rem - 1024) // 2 // 32 * 32
    arena_ap = sb("arena", [128, an], BF16)
    A = Arena(arena_ap, an)

    psf = [nc.alloc_psum_tensor(f"ps{i}", [128, 512], F32).ap() for i in range(7)]
    psB = [Buf(f"ps{i}", excl=True) for i in range(7)]
    pstr = nc.alloc_psum_tensor("pstr", [128, 1024], BF16).ap()
    pstrB = Buf("pstr", excl=True)
    st = {}

    def st_reset():
        st.update({"ps": 0, "w": 0, "wl": 0, "wc": 0, "hu": 0, "first_resid_done": False, "nrot": 5, "rq": "sp"})
    st_reset()

    def bank():
        i = st["ps"] % st["nrot"]
        st["ps"] += 1
        return psf[i], psB[i]

    ident = cB[:, 0:128]
    ones_b = cB[:, 128:256]
    mask2 = cB[:, 256:512]
    perm32 = cB[0:32, 512:544]
    perm128 = cB[:, 512:544]
    negm2 = cB[:, 544:800]
    triU = cF[:, 0:128]
    ones_f = cF[:, 128:256]
    sel127 = cF[:, 256:384]

    P.dma("sp", lambda e: e.dma_start(out=cF, in_=cFd), cFB, writes=[cFB])
    P.dma("sp", lambda e: e.dma_start(out=sv, in_=svd), svB, writes=[svB])
    P.dma("sp", lambda e: e.dma_start(out=bfb, in_=bfd), bfbB, writes=[bfbB])
    A.reset()
    tmpc, tmpcB = A.f32(800, "tmpc")
    P.dma("sp", lambda e: e.dma_start(out=tmpc, in_=cBd), tmpcB, writes=[tmpcB])
    P.op("dve", lambda e: e.tensor_copy(cB, tmpc), reads=[tmpcB], writes=[cBB])
    P.barrier()

    wsched = []
    LA_LOAD, LA_CAST = 3, 2

    def _w_load(i):
        src, kc = wsched[i]
        s_t, s_b = wst[i % NWS], wstB[i % NWS]
        dstv = s_t[:, 0:kc, :].rearrange("p c n -> p (c n)")
        P.dma("sp", lambda e: e.dma_start(out=dstv, in_=src), s_b, writes=[s_b])

    def _w_cast(i):
        src, kc = wsched[i]
        s_t, s_b = wst[i % NWS], wstB[i % NWS]
        w_t, w_b = wbf[i % NWB], wbfB[i % NWB]
        P.op("pool", lambda e: e.tensor_copy(w_t[:, 0:kc, :], s_t[:, 0:kc, :]), reads=[s_b], writes=[w_b])

    def load_w(src, kc):
        i = st["w"]
        st["w"] += 1
        if P.dry:
            wsched.append((src, kc))
            return wbf[i % NWB], wbfB[i % NWB]
        n = len(wsched)
        while st["wl"] < min(n, i + 1 + LA_LOAD):
            while st["wl"] - st["wc"] >= NWS:
                _w_cast(st["wc"])
                st["wc"] += 1
            _w_load(st["wl"])
            st["wl"] += 1
        while st["wc"] < min(n, i + 1 + LA_CAST):
            _w_cast(st["wc"])
            st["wc"] += 1
        return wbf[i % NWB], wbfB[i % NWB]

    def mmgroup(ps, psb, pairs, reads):
        n = len(pairs)
        for i, (l_, r_) in enumerate(pairs):
            P.op("pe", lambda e, l_=l_, r_=r_, i=i: e.matmul(ps, l_, r_, start=(i == 0), stop=(i == n - 1)),
                 reads=reads, writes=[psb])

    def hsrc_ap():
        return hT if st["first_resid_done"] else xT

    pend_hu = {}

    def resid_prefetch(c):
        for tt in range(NT):
            i = st["hu"] % NHU
            st["hu"] += 1
            h_t, h_b = hu[i], huB[i]
            src = hsrc_ap()[c * 128:(c + 1) * 128, tt * TT:(tt + 1) * TT]
            hb = hTB[c][tt]
            P.dma(st["rq"], lambda e, h_t=h_t, src=src: e.dma_start(out=h_t, in_=src), h_b, reads=[hb], writes=[h_b])
            pend_hu[(c, tt)] = (h_t, h_b)

    def resid(c, tt, delta, dbufs):
        h_t, h_b = pend_hu.pop((c, tt))
        dst = hT[c * 128:(c + 1) * 128, tt * TT:(tt + 1) * TT]
        hb = hTB[c][tt]
        P.op("dve", lambda e: e.tensor_tensor(h_t, delta, h_t, ALU.add), reads=[h_b] + dbufs, writes=[h_b])
        return P.dma(st["rq"], lambda e: e.dma_start(out=dst, in_=h_t), h_b, reads=[h_b], writes=[hb])

    def tsl(tt):
        return slice(tt * TT, (tt + 1) * TT)

    def norm_phase(gcol0, final=False):
        A.reset()
        SUB = 256
        NS = S // SUB
        hld = [A.f32(NCH * SUB, "hld") for _ in range(4)]
        sq = [A.bf(SUB, "sq") for _ in range(4)]
        rsl = [A.f32(SUB, "rs") for _ in range(2)]
        outs = []
        if final:
            ot = [A.f32(SUB, "ot") for _ in range(4)]
        src = hsrc_ap().rearrange("(c p) t -> p c t", p=128)
        for s_ in range(NS):
            tt = s_ // 2
            ts_ = slice(s_ * SUB, (s_ + 1) * SUB)
            h_t, h_b = hld[s_ % 4]
            rs, rsB = rsl[s_ % 2]
            h3 = h_t.rearrange("p (c t) -> p c t", c=NCH)
            P.dma("sp" if s_ % 2 == 0 else "act", lambda e, h3=h3, ts_=ts_: e.dma_start(out=h3, in_=src[:, :, ts_]), h_b,
                  reads=[hTB[c][tt] for c in range(NCH)], writes=[h_b])
            ps, psb = bank()
            for c in range(NCH):
                q_t, q_b = sq[c % 4]
                P.op("act", lambda e, q_t=q_t, c=c, h3=h3: e.activation(q_t, h3[:, c, :], AF.Square),
                     reads=[h_b], writes=[q_b])
                P.op("pe", lambda e, q_t=q_t, c=c, ps=ps: e.matmul(ps[:, 0:SUB], ones_b, q_t, start=(c == 0), stop=(c == NCH - 1)),
                     reads=[q_b, cBB], writes=[psb])
            P.op("dve", lambda e, ps=ps, rs=rs: e.tensor_scalar(rs, ps[:, 0:SUB], 1.0 / D, EPS, ALU.mult, ALU.add), reads=[psb], writes=[rsB])
            P.op("act", lambda e, rs=rs: e.activation(rs, rs, AF.Sqrt), reads=[rsB], writes=[rsB])
            P.op("dve", lambda e, rs=rs: e.reciprocal(rs, rs), reads=[rsB], writes=[rsB])
            for c in range(NCH):
                gcol = sv[:, gcol0 + c:gcol0 + c + 1]
                if not final:
                    P.op("dve", lambda e, c=c, h3=h3, gcol=gcol, ts_=ts_, rs=rs: e.scalar_tensor_tensor(
                        XT[:, c, ts_], h3[:, c, :], gcol, rs, ALU.mult, ALU.mult),
                        reads=[h_b, rsB, svB], writes=[XB[tt]])
                else:
                    o_t, o_b = ot[c % 4]
                    P.op("dve", lambda e, c=c, h3=h3, gcol=gcol, o_t=o_t, rs=rs: e.scalar_tensor_tensor(
                        o_t, h3[:, c, :], gcol, rs, ALU.mult, ALU.mult),
                        reads=[h_b, rsB, svB], writes=[o_b])
                    outs.append(P.dma("sp", lambda e, c=c, ts_=ts_, o_t=o_t: e.dma_start(
                        out=outT[c * 128:(c + 1) * 128, ts_], in_=o_t), o_b, reads=[o_b]))
        P.barrier()
        return outs

    def mixer(l):
        A.reset()
        qc, _ = A.bf(S, "qc")
        kc, _ = A.bf(S, "kc")
        vc, _ = A.bf(S, "vc")
        Vt, _ = A.bf(S, "V")
        qcB = [Buf(f"qc{t_}") for t_ in range(NT)]
        kcB = [Buf(f"kc{t_}") for t_ in range(NT)]
        vcB = [Buf(f"vc{t_}") for t_ in range(NT)]
        VB = [Buf(f"V{t_}") for t_ in range(4)]
        V3 = Vt.rearrange("p (t d) -> p t d", d=128)
        nd, ndB = A.f32(2 * S, "nd")
        nd3 = nd.rearrange("p (a t) -> p a t", a=2)
        ndG = [Buf("ndG0"), Buf("ndG1"), Buf("ndG2")]
        rope, ropeB = A.f32(2 * S, "rope")
        ropeC = rope[0:32, 0:S]
        ropeS = rope[0:32, S:2 * S]
        yat = [A.bf(S, "yat") for _ in range(2)]
        ptd = [A.bf(256, "ptd") for _ in range(8)]
        ptf = [A.bf(512, "ptf") for _ in range(6)]
        t1, t1B = A.f32(TT, "t1")
        t2, t2B = A.f32(TT, "t2")
        qn2 = [A.bf(TT, "qn") for _ in range(2)]
        rd, rdB = A.f32(TT, "rd")
        biasT, biasB = A.f32(256, "biasT")
        fb, fbB = A.f32(64, "fb")
        nls, nlsB = A.f32(64, "nls")
        lsx, lsxB = A.f32(64, "lsx")
        ncs, ncsB = A.f32(64, "ncs")
        ncref, ncrefB = A.f32(64, "ncref")
        wfs, wfsB = A.f32(64, "wfs")
        wfb, wfbB = A.bf(64, "wfb")
        wfs3 = wfs.rearrange("p (c n) -> p c n", n=4)
        wfb3 = wfb.rearrange("p (c n) -> p c n", n=4)

        P.dma("sp", lambda e: e.dma_start(out=rope[0:32, :], in_=roped), ropeB, writes=[ropeB])
        P.dma("sp", lambda e: e.dma_start(out=wfs3, in_=wfd[l].rearrange("(c p) n -> p c n", p=128)), wfsB, writes=[wfsB])
        P.op("dve", lambda e: e.tensor_copy(wfb, wfs), reads=[wfsB], writes=[wfbB])
        psF, psFB = bank()
        for t16 in range(16):
            for k in range(NCH):
                P.op("pe", lambda e, t16=t16, k=k: e.matmul(
                    psF[:, t16 * 4:(t16 + 1) * 4], XT[:, k, t16 * 128:(t16 + 1) * 128], wfb3[:, k, :],
                    start=(k == 0), stop=(k == NCH - 1)), reads=[XB[t16 // 4], wfbB], writes=[psFB])
        P.op("dve", lambda e: e.tensor_tensor(fb, psF[:, 0:64], bfb[:, l * 64:(l + 1) * 64], ALU.add),
             reads=[psFB, bfbB], writes=[fbB])
        P.op("act", lambda e: e.activation(fb, fb, AF.Exp, scale=-1.0), reads=[fbB], writes=[fbB])
        P.op("dve", lambda e: e.tensor_scalar_add(fb, fb, 1.0), reads=[fbB], writes=[fbB])
        P.op("act", lambda e: e.activation(nls, fb, AF.Ln), reads=[fbB], writes=[nlsB])
        P.op("dve", lambda e: e.memset(lsx[:, 0:4], 0.0), writes=[lsxB])
        for t in range(1, 16):
            P.op("dve", lambda e, t=t: e.tensor_tensor(lsx[:, t * 4:(t + 1) * 4], lsx[:, (t - 1) * 4:t * 4],
                                                       nls[:, (t - 1) * 4:t * 4], ALU.add),
                 reads=[lsxB, nlsB], writes=[lsxB])
        psC, psCB = bank()
        P.op("pe", lambda e: e.matmul(psC[:, 0:64], triU, nls, start=True, stop=False), reads=[nlsB, cFB], writes=[psCB])
        P.op("pe", lambda e: e.matmul(psC[:, 0:64], ones_f, lsx, start=False, stop=True), reads=[lsxB, cFB], writes=[psCB])
        P.op("dve", lambda e: e.tensor_copy(ncs, psC[:, 0:64]), reads=[psCB], writes=[ncsB])
        psR, psRB = bank()
        P.op("pe", lambda e: e.matmul(psR[:, 0:64], sel127, ncs, start=True, stop=True), reads=[ncsB, cFB], writes=[psRB])
        P.op("dve", lambda e: e.tensor_copy(ncref, psR[:, 0:64]), reads=[psRB], writes=[ncrefB])
        if sub == "prelude":
            P.barrier()
            return

        def tts_of(d, lo, hi):
            Lc = S // d
            r, m0 = divmod(lo, Lc)
            m1 = m0 + (hi - lo)
            return list(range((m0 * d) // TT, min(NT - 1, ((m1 - 1) * d + r) // TT) + 1))

        def project(hd, d, rope_on):
            pend = []
            gi = 0
            trq = []
            for pi, (dst, dstBl) in ((2, (vc, vcB)), (1, (kc, kcB)), (0, (qc, qcB))):
                w_t, w_b = load_w(whead[l, hd, pi], NCH)
                d3 = dst.rearrange("p (r m) -> p r m", r=d)
                for tt in range(NT):
                    ps, psb = bank()
                    mmgroup(ps, psb, [(w_t[:, k, :], XT[:, k, tsl(tt)]) for k in range(NCH)], [w_b, XB[tt]])
                    n = TT // d
                    dv = d3[:, :, tt * n:(tt + 1) * n]
                    sv_ = ps.rearrange("p (m r) -> p r m", r=d)
                    P.op("act", lambda e, dv=dv, sv_=sv_: e.activation(dv, sv_, AF.Copy), reads=[psb], writes=[dstBl[tt]])
                    tail = None
                    if rope_on and pi < 2:
                        q_t, q_b = qn2[gi % 2]
                        gi += 1
                        P.op("act", lambda e, ps=ps, q_t=q_t: e.activation(q_t, ps, AF.Copy), reads=[psb], writes=[q_b])

                        def tail(ps=ps, psb=psb, q_t=q_t, q_b=q_b, tt=tt, d3=d3, dstB=dstBl[tt], n=n):
                            pw, pwb = bank()
                            P.op("pe", lambda e: e.matmul(pw[0:32, :], perm128, q_t, start=True, stop=True),
                                 reads=[q_b, cBB], writes=[pwb])
                            P.op("dve", lambda e: e.tensor_tensor(t1[0:32, :], ps[0:32, :], ropeC[:, tsl(tt)], ALU.mult),
                                 reads=[psb, ropeB], writes=[t1B])
                            P.op("dve", lambda e: e.tensor_tensor(t2[0:32, :], pw[0:32, :], ropeS[:, tsl(tt)], ALU.mult),
                                 reads=[pwb, ropeB], writes=[t2B])
                            t1v = t1[0:32, :].rearrange("p (m r) -> p r m", r=d)
                            t2v = t2[0:32, :].rearrange("p (m r) -> p r m", r=d)
                            dv32 = d3[0:32, :, tt * n:(tt + 1) * n]
                            P.op("dve", lambda e: e.tensor_tensor(dv32, t1v, t2v, ALU.add),
                                 reads=[t1B, t2B], writes=[dstB])
                    for f_ in pend:
                        f_()
                    pend = [tail] if tail is not None else []
                    if pi == 2 and tt == NT - 1:
                        trq = [0, 1, 2, 3]
                    elif pi == 1 and trq:
                        i4 = trq.pop(0)
                        half = (i4 % 2) * 512
                        for ii in range(4):
                            i = i4 * 4 + ii
                            P.op("pe", lambda e, i=i, ii=ii, half=half: e.transpose(
                                pstr[:, half + ii * 128:half + (ii + 1) * 128], vc[:, i * 128:(i + 1) * 128], ident),
                                reads=[vcB[t_] for t_ in tts_of(d, i * 128, (i + 1) * 128)] + [cBB], writes=[pstrB])
                        P.op("act", lambda e, i4=i4, half=half: e.activation(
                            V3[:, i4 * 4:(i4 + 1) * 4, :], pstr[:, half:half + 512].rearrange("p (t d) -> p t d", d=128), AF.Copy),
                            reads=[pstrB], writes=[VB[i4]])
            for f_ in pend:
                f_()

        for j in range(4):
            for g in range(3):
                d = DILS[g]
                nb = 16 // d
                project(g * 4 + j, d, True)
                if sub == "proj":
                    P.barrier()
                    return
                nd4 = nd3.rearrange("p a (m r) -> p a r m", r=d)
                pts = {}

                def dil_s1(i, nb=nb, d_=d):
                    r, n = divmod(i, nb)
                    nq = 256 if n < nb - 1 else 128
                    pt, ptB = ptd[i % 8]
                    pts[i] = (pt, ptB)
                    ps, psb = bank()
                    P.op("pe", lambda e: e.matmul(ps[:, 0:nq], kc[:, i * 128:(i + 1) * 128],
                                                  qc[:, i * 128:i * 128 + nq], start=True, stop=True),
                         reads=[kcB[t_] for t_ in tts_of(d_, i * 128, (i + 1) * 128)]
                         + [qcB[t_] for t_ in tts_of(d_, i * 128, i * 128 + nq)], writes=[psb])
                    P.op("act", lambda e: e.activation(pt[:, 0:nq], ps[:, 0:nq], AF.Exp, scale=SCALE),
                         reads=[psb], writes=[ptB])
                    P.op("pool", lambda e: e.tensor_tensor(pt[:, 0:nq], pt[:, 0:nq], mask2[:, 0:nq], ALU.mult),
                         reads=[ptB, cBB], writes=[ptB])

                def dil_s2(i, nb=nb, g=g, nd4=nd4):
                    r, n = divmod(i, nb)
                    pt, ptB = pts[i]
                    po, pob = bank()
                    pairs_o = [(V3[:, i, :], pt[:, 0:128])]
                    pairs_d = [(ones_b, pt[:, 0:128])]
                    rds = [VB[i // 4], ptB, cBB]
                    if n > 0:
                        ppt, pptB = pts[i - 1]
                        pairs_o.append((V3[:, i - 1, :], ppt[:, 128:256]))
                        pairs_d.append((ones_b, ppt[:, 128:256]))
                        rds = rds + [pptB, VB[(i - 1) // 4]]
                    mmgroup(po[:, 0:128], pob, pairs_o, rds)
                    mmgroup(po[:, 128:256], pob, pairs_d, rds)
                    dstv = nd4[:, :, r, n * 128:(n + 1) * 128]
                    srcv = po[:, 0:256].rearrange("p (a q) -> p a q", a=2)
                    if g == 0:
                        P.op("dve", lambda e: e.tensor_copy(dstv, srcv), reads=[pob], writes=[ndG[0]])
                    else:
                        P.op("dve", lambda e: e.tensor_tensor(dstv, srcv, dstv, ALU.add),
                             reads=[pob, ndG[g - 1]], writes=[ndG[g]])

                if 'noattn' not in os.environ.get('KDBG', ''):
                    LAD = 5
                    for i in range(16 + LAD):
                        if i < 16:
                            dil_s1(i)
                        if i >= LAD:
                            dil_s2(i - LAD)
            if sub == "dil":
                P.barrier()
                return
            y_t, y_b = yat[0]
            P.op("dve", lambda e: e.reciprocal(nd3[:, 1, :], nd3[:, 1, :]), reads=[ndG[2]], writes=[ndG[2]])
            P.op("dve", lambda e, y_t=y_t: e.tensor_tensor(y_t, nd3[:, 0, :], nd3[:, 1, :], ALU.mult), reads=[ndG[2]], writes=[y_b])
            P.dma("sp", lambda e, y_t=y_t, j=j: e.dma_start(out=yab[j * 128:(j + 1) * 128, :], in_=y_t), y_b,
                  reads=[y_b], writes=[yabB[j]])

            if sub == "dilout":
                P.barrier()
                return
            project(12 + j, 1, False)
            for kb in range(16):
                P.op("dve", lambda e, kb=kb, j=j: e.tensor_scalar(
                    biasT[:, kb * 16:(kb + 1) * 16], ncref[:, j:64:4], ncs[:, kb * 4 + j:kb * 4 + j + 1], -1.0,
                    ALU.subtract, ALU.mult), reads=[ncrefB, ncsB], writes=[biasB])
            y_t, y_b = yat[1]
            its = [(ch, kb) for ch in range(4) for kb in range(4 * ch + 4)]
            fpt = {}

            def fox_s1(t, j=j):
                ch, kb = its[t]
                irel = max(0, kb - 4 * ch)
                qlo = irel * 128
                nq = 512 - qlo
                pt, ptB = ptf[t % 6]
                fpt[t] = (pt, ptB)
                ps, psb = bank()
                P.op("pe", lambda e: e.matmul(
                    ps[:, 0:nq], kc[:, kb * 128:(kb + 1) * 128], qc[:, ch * 512 + qlo:(ch + 1) * 512],
                    start=True, stop=True), reads=[kcB[kb // 4], qcB[ch]], writes=[psb])
                P.op("act", lambda e: e.activation(
                    pt[:, 0:nq], ps[:, 0:nq], AF.Exp,
                    bias=biasT[:, kb * 16 + 4 * ch + 1:kb * 16 + 4 * ch + 2], scale=SCALE),
                    reads=[psb, biasB], writes=[ptB])
                if kb >= 4 * ch:
                    P.op("pool", lambda e: e.tensor_tensor(pt[:, 0:128], pt[:, 0:128], mask2[:, 0:128], ALU.mult),
                         reads=[ptB, cBB], writes=[ptB])

            def fox_s2(t, y_t=y_t, y_b=y_b):
                ch, kb = its[t]
                nkb = 4 * ch + 4
                irel = max(0, kb - 4 * ch)
                qlo = irel * 128
                nq = 512 - qlo
                pt, ptB = fpt[t]
                po, pob = (psf[5], psB[5]) if ch % 2 == 0 else (psf[3], psB[3])
                pd, pdb = (psf[6], psB[6]) if ch % 2 == 0 else (psf[4], psB[4])
                first, last = (kb == 0), (kb == nkb - 1)
                P.op("pe", lambda e: e.matmul(po[:, qlo:512], V3[:, kb, :], pt[:, 0:nq], start=first, stop=last),
                     reads=[VB[kb // 4], ptB], writes=[pob])
                P.op("pe", lambda e: e.matmul(pd[:, qlo:512], ones_b, pt[:, 0:nq], start=first, stop=last),
                     reads=[cBB, ptB], writes=[pdb])
                if last:
                    P.op("dve", lambda e: e.reciprocal(rd, pd), reads=[pdb], writes=[rdB])
                    P.op("dve", lambda e: e.tensor_tensor(y_t[:, ch * 512:(ch + 1) * 512], po, rd, ALU.mult),
                         reads=[pob, rdB], writes=[y_b])

            LA = 4
            st["nrot"] = 3
            for t in range(len(its) + LA if 'noattn' not in os.environ.get('KDBG', '') else 0):
                if t < len(its):
                    fox_s1(t)
                if t >= LA:
                    fox_s2(t - LA)
            st["nrot"] = 5
            P.dma("sp", lambda e, y_t=y_t, j=j: e.dma_start(out=yab[512 + j * 128:512 + (j + 1) * 128, :], in_=y_t), y_b,
                  reads=[y_b], writes=[yabB[4 + j]])
        P.barrier()

    def merge_phase(l):
        A.reset()
        mg, mgB = A.bf(8 * S, "mg")
        mg3 = mg.rearrange("p (c t) -> p c t", c=8)
        yh, yhB = A.bf(8 * S, "yh")
        yh3 = yh.rearrange("p (c t) -> p c t", c=8)
        ga = [A.f32(TT, "ga") for _ in range(NT)]
        gb = [A.f32(TT, "gb") for _ in range(NT)]
        bg0 = 80 * l + 48
        P.dma("sp", lambda e: e.dma_start(out=yh3, in_=yab.rearrange("(c p) t -> p c t", p=128)),
              yhB, reads=yabB, writes=[yhB])
        for hh in range(2):
            for n8 in range(8):
                n = hh * 8 + n8
                wga, wgab = load_w(w_gate[l, n], NCH)
                for tt in range(NT):
                    pga, pgab = bank()
                    mmgroup(pga, pgab, [(wga[:, k, :], XT[:, k, tsl(tt)]) for k in range(NCH)], [wgab, XB[tt]])
                    ga_t, ga_b = ga[tt]
                    P.op("act", lambda e, ga_t=ga_t, pga=pga, n=n: e.activation(
                        ga_t, pga, AF.Sigmoid, bias=sv[:, bg0 + n:bg0 + n + 1]), reads=[pgab, svB], writes=[ga_b])
                wgb, wgbb = load_w(w_gate[l, 16 + n], NCH)
                for tt in range(NT):
                    pgb, pgbb = bank()
                    mmgroup(pgb, pgbb, [(wgb[:, k, :], XT[:, k, tsl(tt)]) for k in range(NCH)], [wgbb, XB[tt]])
                    gb_t, gb_b = gb[tt]
                    P.op("act", lambda e, gb_t=gb_t, pgb=pgb, n=n: e.activation(
                        gb_t, pgb, AF.Sigmoid, bias=sv[:, bg0 + 16 + n:bg0 + 16 + n + 1]), reads=[pgbb, svB], writes=[gb_b])
                wa, wab = load_w(w_br_a[l, n], 4)
                for tt in range(NT):
                    pa, pab = bank()
                    mmgroup(pa, pab, [(wa[:, jj, :], yh3[:, jj, tsl(tt)]) for jj in range(4)], [wab, yhB])
                    ga_t, ga_b = ga[tt]
                    P.op("dve", lambda e, pa=pa, ga_t=ga_t: e.tensor_tensor(ga_t, pa, ga_t, ALU.mult),
                         reads=[pab, ga_b], writes=[ga_b])
                wb_, wbb = load_w(w_br_b[l, n], 4)
                for tt in range(NT):
                    pb, pbb = bank()
                    mmgroup(pb, pbb, [(wb_[:, jj, :], yh3[:, 4 + jj, tsl(tt)]) for jj in range(4)], [wbb, yhB])
                    ga_t, ga_b = ga[tt]
                    gb_t, gb_b = gb[tt]
                    P.op("dve", lambda e, pb=pb, gb_t=gb_t: e.tensor_tensor(gb_t, pb, gb_t, ALU.mult),
                         reads=[pbb, gb_b], writes=[gb_b])
                    P.op("pool", lambda e, n8=n8, tt=tt, ga_t=ga_t, gb_t=gb_t: e.tensor_tensor(mg3[:, n8, tsl(tt)], ga_t, gb_t, ALU.add),
                         reads=[ga_b, gb_b], writes=[mgB])
            st["rq"] = "act"
            resid_prefetch(0)
            for c in range(NCH):
                wo, wob = load_w(w_o[l, c, :, hh * 1024:(hh + 1) * 1024], 8)
                if c + 1 < NCH:
                    resid_prefetch(c + 1)
                for tt in range(NT):
                    ps, psb = bank()
                    mmgroup(ps, psb, [(wo[:, n8, :], mg3[:, n8, tsl(tt)]) for n8 in range(8)], [wob, mgB])
                    resid(c, tt, ps, [psb])
            st["rq"] = "sp"
            if hh == 0:
                st["first_resid_done"] = True
        P.barrier()

    def mlp_phase(l):
        A.reset()
        at, atB = A.bf(NCH * S, "at")
        at3 = at.rearrange("p (f t) -> p f t", f=NCH)
        rl = [A.f32(TT, "rl") for _ in range(2)]
        it = 0
        for g in range(4):
            for f in range(NCH):
                fc = g * NCH + f
                wu, wub = load_w(w_up[l, fc], NCH)
                for tt in range(NT):
                    ps, psb = bank()
                    mmgroup(ps, psb, [(wu[:, k, :], XT[:, k, tsl(tt)]) for k in range(NCH)], [wub, XB[tt]])
                    r_t, r_b = rl[it % 2]
                    it += 1
                    P.op("act", lambda e, r_t=r_t, ps=ps: e.activation(r_t, ps, AF.Relu), reads=[psb], writes=[r_b])
                    P.op("dve", lambda e, r_t=r_t, f=f, tt=tt: e.tensor_tensor(at3[:, f, tsl(tt)], r_t, r_t, ALU.mult),
                         reads=[r_b], writes=[atB])
            st["rq"] = "act"
            resid_prefetch(0)
            for c in range(NCH):
                wd, wdb = load_w(w_down[l, g, c], NCH)
                if c + 1 < NCH:
                    resid_prefetch(c + 1)
                for tt in range(NT):
                    ps, psb = bank()
                    mmgroup(ps, psb, [(wd[:, f, :], at3[:, f, tsl(tt)]) for f in range(NCH)], [wdb, atB])
                    resid(c, tt, ps, [psb])
            st["rq"] = "sp"
        P.barrier()

    def ple_phase(l):
        A.reset()
        pst_, pstB_ = A.f32(2 * S, "pst")
        pb_, pbB_ = A.bf(2 * S, "pb")
        pst3 = pst_.rearrange("p (c t) -> p c t", c=2)
        pb3 = pb_.rearrange("p (c t) -> p c t", c=2)
        gt = [A.f32(TT, "gt") for _ in range(2)]
        P.dma("sp", lambda e: e.dma_start(out=pst3, in_=pT[l].rearrange("(c p) t -> p c t", p=128)), pstB_, writes=[pstB_])
        P.op("dve", lambda e: e.tensor_copy(pb_, pst_), reads=[pstB_], writes=[pbB_])
        it = 0
        resid_prefetch(0)
        for c in range(NCH):
            if c + 1 < NCH:
                resid_prefetch(c + 1)
            wg, wgb_ = load_w(w_pg[l, c], NCH)
            wp, wpb = load_w(w_ple[l, c], 2)
            for tt in range(NT):
                pg, pgb_ = bank()
                mmgroup(pg, pgb_, [(wg[:, k, :], XT[:, k, tsl(tt)]) for k in range(NCH)], [wgb_, XB[tt]])
                pp, ppb = bank()
                mmgroup(pp, ppb, [(wp[:, k, :], pb3[:, k, tsl(tt)]) for k in range(2)], [wpb, pbB_])
                g_t, g_b = gt[it % 2]
                it += 1
                P.op("act", lambda e, g_t=g_t, pg=pg: e.activation(g_t, pg, AF.Sigmoid), reads=[pgb_], writes=[g_b])
                P.op("dve", lambda e, g_t=g_t, pp=pp: e.tensor_tensor(g_t, pp, g_t, ALU.mult), reads=[ppb, g_b], writes=[g_b])
                resid(c, tt, g_t, [g_b])
        P.barrier()

    def run_all():
        for l in range(n_layers):
            norm_phase(80 * l + 0)
            if stop_after == "norm":
                break
            mixer(l)
            if stop_after == "mixer":
                break
            merge_phase(l)
            if stop_after == "merge":
                break
            norm_phase(80 * l + 16)
            mlp_phase(l)
            if stop_after == "mlp":
                break
            norm_phase(80 * l + 32)
            ple_phase(l)
        return norm_phase(320, final=True)

    P.dry = True
    run_all()
    P.dry = False
    st_reset()
    finals = run_all()
    P.emit(final_wait_ops=finals)
    return nc, P


def _consts():
    k = np.arange(128)
    triU = (k[:, None] <= k[None, :]).astype(np.float32)
    ones = np.ones((128, 128), np.float32)
    sel = np.zeros((128, 128), np.float32)
    sel[127, :] = 1.0
    constF = np.concatenate([triU, ones, sel], axis=1)
    ident = np.eye(128, dtype=np.float32)
    m_cur = (k[:, None] <= k[None, :]).astype(np.float32)
    m_prev = (k[:, None] >= k[None, :]).astype(np.float32)
    perm = np.zeros((128, 32), np.float32)
    for m in range(32):
        perm[(m + 16) % 32, m] = 1.0
    negm = ((1.0 - np.concatenate([m_cur, m_prev], axis=1)) * -30000.0).astype(np.float32)
    constB = np.concatenate([ident, ones, m_cur, m_prev, perm, negm], axis=1)
    half = 16
    inv = (500000.0 ** (-np.arange(half, dtype=np.float32) / half)).astype(np.float32)
    ang = (np.arange(S, dtype=np.float32)[:, None] * inv[None, :]).astype(np.float32)
    cos = np.cos(ang).astype(np.float32).T
    sin = np.sin(ang).astype(np.float32).T
    C = np.concatenate([cos, cos], axis=0)
    Sn = np.concatenate([-sin, sin], axis=0)
    rope = np.concatenate([C, Sn], axis=1).astype(np.float32)
    return constF, constB, rope


_CACHE = {}


def _prep_shared(inp):
    f = np.float32
    w_in = np.asarray(inp["w_in"], f)
    def tiled(w, kc):
        n = w.shape[2] // 128
        return np.ascontiguousarray(np.asarray(w, f).reshape(L, kc, 128, n, 128).transpose(0, 3, 2, 1, 4).reshape(L, n, 128, kc * 128))
    whead = np.ascontiguousarray(
        w_in[:, :, :6144].reshape(L, 16, 128, 3, 16, 128).transpose(0, 4, 3, 2, 1, 5).reshape(L, 16, 3, 128, 2048))
    wf = np.ascontiguousarray(w_in[:, :, 6144:6148])
    sv = np.zeros((128, 336), f)
    for l in range(L):
        sv[:, 80 * l + 0:80 * l + 16] = np.asarray(inp["g_mix"][l], f).reshape(16, 128).T
        sv[:, 80 * l + 16:80 * l + 32] = np.asarray(inp["g_mlp"][l], f).reshape(16, 128).T
        sv[:, 80 * l + 32:80 * l + 48] = np.asarray(inp["g_ple"][l], f).reshape(16, 128).T
        sv[:, 80 * l + 48:80 * l + 80] = np.asarray(inp["b_gate"][l], f).reshape(32, 128).T
    sv[:, 320:336] = np.asarray(inp["g_final"], f).reshape(16, 128).T
    bf = np.asarray(inp["b_f"], f)
    bf_bc = np.ascontiguousarray(np.broadcast_to(bf[None, :, None, :], (128, L, 16, 4)).reshape(128, L * 64))
    constF, constB, rope = _consts()
    shared = {
        "whead": whead, "wf": wf,
        "w_gate": tiled(inp["w_gate"], 16), "w_br_a": tiled(inp["w_br_a"], 4),
        "w_br_b": tiled(inp["w_br_b"], 4), "w_o": tiled(inp["w_o"], 16),
        "w_up": tiled(inp["w_up"], 16),
        "w_down": np.ascontiguousarray(np.asarray(inp["w_down"], f).reshape(L, 4, 16, 128, 16, 128).transpose(0, 1, 4, 3, 2, 5).reshape(L, 4, 16, 128, 2048)),
        "w_ple": tiled(inp["w_ple"], 2), "w_pg": tiled(inp["w_ple_gate"], 16),
        "smallvec": sv, "bf_bc": bf_bc, "constF": constF, "constB": constB, "ropeCS": rope,
    }
    return shared


def kernel(**inputs):
    x = np.asarray(inputs["x"], np.float32)
    p = np.asarray(inputs["p"], np.float32)
    shared = _prep_shared(inputs)
    if "nc" not in _CACHE:
        _CACHE["nc"] = build()[0]
    nc = _CACHE["nc"]
    in_maps = []
    for b in range(8):
        m = dict(shared)
        m["xT"] = np.ascontiguousarray(x[b].T)
        m["pT"] = np.ascontiguousarray(p[:, b].transpose(0, 2, 1))
        in_maps.append(m)
    res = run_bass_kernel_spmd(nc, in_maps, core_ids=list(range(8)))
    out = np.stack([np.ascontiguousarray(res.results[b]["outT"].T) for b in range(8)], axis=0)
    return out.astype(np.float32)
```
